# Optimizing a Trainium2 kernel written in Bass

```python
import math
import jax
import jax.numpy as jnp
from jax import lax
import numpy as np

D_MODEL = 2048
BATCH = 2
SEQ = 8192
DEPTH = 2

GRID_W = 64
CTX_LEN = 256
EPS = 1e-6

SSM_WIDTH = 512
SSM_GROUP = 16
SSM_GROUPS = SSM_WIDTH // SSM_GROUP
SSM_STATE = 64
DT_MIN = 1e-3
DT_MAX = 1e-1

ATTN_HEADS = 8
ATTN_KV_HEADS = 2
HEAD_DIM = 128
ATTN_WIDTH = ATTN_HEADS * HEAD_DIM
KV_WIDTH = ATTN_KV_HEADS * HEAD_DIM
Q_BLOCK = 128
ROPE_THETA = 10000.0

MLSTM_HEADS = 4
MLSTM_HEAD_DIM = 128
MLSTM_WIDTH = MLSTM_HEADS * MLSTM_HEAD_DIM
MLSTM_CHUNK = 128
CONV_WIDTH = 3
N_GATES = 4 * MLSTM_HEADS

SECTION_SIZES = (SSM_WIDTH, SSM_WIDTH,
                 ATTN_WIDTH, KV_WIDTH, KV_WIDTH, ATTN_WIDTH,
                 MLSTM_WIDTH, MLSTM_WIDTH, MLSTM_WIDTH, MLSTM_WIDTH, N_GATES, MLSTM_WIDTH)
IN_COLS = sum(SECTION_SIZES)
MIX_WIDTH = SSM_WIDTH + ATTN_WIDTH + MLSTM_WIDTH

kernel_name = 'hybrid_s5_gqa_mlstm_dit_block'


def rmsnorm(x, w):
    xf = x.astype(jnp.float32)
    y = xf * lax.rsqrt(jnp.mean(xf * xf, axis=-1, keepdims=True) + EPS)
    return (y * w.astype(jnp.float32)).astype(x.dtype)


def split_sections(p):
    bounds = []
    acc = 0
    for s in SECTION_SIZES[:-1]:
        acc += s
        bounds.append(acc)
    return jnp.split(p, bounds, axis=-1)


def maybe_flip(t, flip, axis):
    return jnp.flip(t, axis=axis) if flip else t


def axial_rope_tables(n_tokens):
    rows = n_tokens // GRID_W
    row = jnp.broadcast_to(jnp.arange(rows, dtype=jnp.float32)[:, None], (rows, GRID_W)).reshape(-1)
    col = jnp.broadcast_to(jnp.arange(GRID_W, dtype=jnp.float32)[None, :], (rows, GRID_W)).reshape(-1)
    n_freq = HEAD_DIM // 4
    inv = ROPE_THETA ** (-jnp.arange(n_freq, dtype=jnp.float32) / n_freq)
    ang = jnp.concatenate([row[:, None] * inv, col[:, None] * inv], axis=-1)
    return jnp.cos(ang), jnp.sin(ang)


def apply_rope(x, cos, sin):
    half = x.shape[-1] // 2
    x1, x2 = x[..., :half], x[..., half:]
    cs, sn = cos[None, :, None, :], sin[None, :, None, :]
    return jnp.concatenate([x1 * cs - x2 * sn, x1 * sn + x2 * cs], axis=-1).astype(x.dtype)


def s5_discretize(a_re, a_im, log_dt, b_re, b_im):
    a_re = a_re.astype(jnp.float32)
    a_im = a_im.astype(jnp.float32)
    dt = jnp.exp(log_dt.astype(jnp.float32))[:, None]
    mag = jnp.exp(a_re * dt)
    lam_re = mag * jnp.cos(a_im * dt)
    lam_im = mag * jnp.sin(a_im * dt)
    inv_abs2 = 1.0 / (a_re * a_re + a_im * a_im)
    num_re, num_im = lam_re - 1.0, lam_im
    f_re = (num_re * a_re + num_im * a_im) * inv_abs2
    f_im = (num_im * a_re - num_re * a_im) * inv_abs2
    b_re = b_re.astype(jnp.float32)
    b_im = b_im.astype(jnp.float32)
    bb_re = f_re[:, None, :] * b_re - f_im[:, None, :] * b_im
    bb_im = f_re[:, None, :] * b_im + f_im[:, None, :] * b_re
    return lam_re, lam_im, bb_re, bb_im


def complex_affine_combine(e1, e2):
    a1r, a1i, b1r, b1i = e1
    a2r, a2i, b2r, b2i = e2
    return (a1r * a2r - a1i * a2i,
            a1r * a2i + a1i * a2r,
            a2r * b1r - a2i * b1i + b2r,
            a2r * b1i + a2i * b1r + b2i)


def s5_scan(u, h_re, h_im, lam_re, lam_im, bb_re, bb_im, c_re, c_im):
    n = u.shape[1]
    x_re = jnp.einsum('blgp,gpn->blgn', u, bb_re)
    x_im = jnp.einsum('blgp,gpn->blgn', u, bb_im)
    x_re = x_re.at[:, 0].add(lam_re * h_re - lam_im * h_im)
    x_im = x_im.at[:, 0].add(lam_re * h_im + lam_im * h_re)
    a_re = jnp.broadcast_to(lam_re, (1, n) + lam_re.shape)
    a_im = jnp.broadcast_to(lam_im, (1, n) + lam_im.shape)
    _, _, s_re, s_im = lax.associative_scan(complex_affine_combine, (a_re, a_im, x_re, x_im), axis=1)
    y = (jnp.einsum('blgn,gpn->blgp', s_re, c_re.astype(jnp.float32))
         - jnp.einsum('blgn,gpn->blgp', s_im, c_im.astype(jnp.float32)))
    return y, s_re[:, -1], s_im[:, -1]


def s5_branch(u_lat, z_lat, u_ctx, z_ctx, a_re, a_im, log_dt, b_re, b_im, c_re, c_im, d, w_glu, need_ctx):
    def grp(t):
        return t.astype(jnp.float32).reshape(t.shape[0], t.shape[1], SSM_GROUPS, SSM_GROUP)
    ul, uc = grp(u_lat), grp(u_ctx)
    d_g = d.astype(jnp.float32).reshape(SSM_GROUPS, SSM_GROUP)
    y_lat = d_g * ul
    y_ctx = d_g * uc if need_ctx else None
    bsz = ul.shape[0]
    for direction in range(2):
        rev = direction == 1
        lam_re, lam_im, bb_re, bb_im = s5_discretize(a_re[direction], a_im[direction], log_dt[direction], b_re, b_im)
        zero = jnp.zeros((bsz, SSM_GROUPS, SSM_STATE), jnp.float32)
        yc, h_re, h_im = s5_scan(maybe_flip(uc, rev, 1), zero, zero, lam_re, lam_im, bb_re, bb_im,
                                 c_re[direction], c_im[direction])
        yl, _, _ = s5_scan(maybe_flip(ul, rev, 1), h_re, h_im, lam_re, lam_im, bb_re, bb_im,
                           c_re[direction], c_im[direction])
        y_lat = y_lat + maybe_flip(yl, rev, 1)
        if need_ctx:
            y_ctx = y_ctx + maybe_flip(yc, rev, 1)

    def glu_out(y, z):
        g = jax.nn.gelu(y.reshape(y.shape[0], y.shape[1], SSM_WIDTH)).astype(z.dtype)
        return g * jax.nn.sigmoid(g @ w_glu) * jax.nn.silu(z)

    out_ctx = glu_out(y_ctx, z_ctx) if need_ctx else None
    return glu_out(y_lat, z_lat), out_ctx


def gqa_attend(q, keys, vals):
    s = jnp.einsum('bqkgd,bskd->bkgqs', q, keys).astype(jnp.float32) * (HEAD_DIM ** -0.5)
    p = jax.nn.softmax(s, axis=-1).astype(vals.dtype)
    return jnp.einsum('bkgqs,bskd->bqkgd', p, vals)


def attention_branch(q, k, v, z, q_c, k_c, v_c, z_c, q_norm_w, k_norm_w, cos, sin, need_ctx):
    bsz, n, _ = q.shape
    grp = ATTN_HEADS // ATTN_KV_HEADS

    def heads(t, nh):
        return t.reshape(t.shape[0], t.shape[1], nh, HEAD_DIM)

    ql = apply_rope(rmsnorm(heads(q, ATTN_HEADS), q_norm_w), cos, sin)
    kl = apply_rope(rmsnorm(heads(k, ATTN_KV_HEADS), k_norm_w), cos, sin)
    kc = rmsnorm(heads(k_c, ATTN_KV_HEADS), k_norm_w)
    vc = heads(v_c, ATTN_KV_HEADS)
    keys = jnp.concatenate([kc, kl], axis=1)
    vals = jnp.concatenate([vc, heads(v, ATTN_KV_HEADS)], axis=1)
    nb = n // Q_BLOCK
    qb = ql.reshape(bsz, nb, Q_BLOCK, ATTN_KV_HEADS, grp, HEAD_DIM).swapaxes(0, 1)
    o = lax.map(lambda blk: gqa_attend(blk, keys, vals), qb)
    o = o.swapaxes(0, 1).reshape(bsz, n, ATTN_WIDTH)
    out_lat = o * jax.nn.silu(z)
    out_ctx = None
    if need_ctx:
        qc = rmsnorm(heads(q_c, ATTN_HEADS), q_norm_w)
        qc = qc.reshape(bsz, qc.shape[1], ATTN_KV_HEADS, grp, HEAD_DIM)
        oc = gqa_attend(qc, kc, vc).reshape(bsz, qc.shape[1], ATTN_WIDTH)
        out_ctx = oc * jax.nn.silu(z_c)
    return out_lat, out_ctx


def centred_dwconv(x, w, b):
    pad = CONV_WIDTH // 2
    y = lax.conv_general_dilated(x, w[:, None, :].astype(x.dtype), window_strides=(1,),
                                 padding=[(pad, pad)], dimension_numbers=('NWC', 'WIO', 'NWC'),
                                 feature_group_count=x.shape[-1])
    return y + b


def mlstm_inputs(q, k, v, g, conv_w, conv_b, gate_b):
    qk = jax.nn.silu(centred_dwconv(jnp.concatenate([q, k], axis=-1), conv_w, conv_b))
    q, k = jnp.split(qk, 2, axis=-1)
    bsz, n, _ = q.shape

    def heads(t):
        return t.astype(jnp.float32).reshape(bsz, n, MLSTM_HEADS, MLSTM_HEAD_DIM).transpose(0, 2, 1, 3)

    gates = (g + gate_b).astype(jnp.float32).reshape(bsz, n, 4, MLSTM_HEADS).transpose(2, 0, 3, 1)
    return heads(q), heads(k), heads(v), gates


def mlstm_scan(q, k, v, i_pre, f_pre, state):
    bsz, nh, n, dh = q.shape
    nc = n // MLSTM_CHUNK

    def to_chunks(t):
        return jnp.moveaxis(t.reshape(t.shape[:2] + (nc, MLSTM_CHUNK) + t.shape[3:]), 2, 0)

    qs, ks, vs = to_chunks(q), to_chunks(k * (dh ** -0.5)), to_chunks(v)
    log_f = to_chunks(jax.nn.log_sigmoid(f_pre))
    log_i = to_chunks(i_pre)
    lower = jnp.tril(jnp.ones((MLSTM_CHUNK, MLSTM_CHUNK), dtype=bool))

    def step(carry, inp):
        c_st, n_st, m_st = carry
        qc, kc, vc, lf, li = inp
        b = jnp.cumsum(lf, axis=-1)
        log_d = jnp.where(lower, b[..., :, None] - b[..., None, :] + li[..., None, :], -jnp.inf)
        inter = b + m_st[..., None]
        m_t = jnp.maximum(inter, jnp.max(log_d, axis=-1))
        s = jnp.einsum('bhtd,bhsd->bhts', qc, kc) * jnp.exp(log_d - m_t[..., None])
        w_inter = jnp.exp(inter - m_t)
        num = jnp.einsum('bhts,bhse->bhte', s, vc) + w_inter[..., None] * jnp.einsum('bhed,bhtd->bhte', c_st, qc)
        den = jnp.sum(s, axis=-1) + w_inter * jnp.einsum('bhtd,bhd->bht', qc, n_st)
        h = num / jnp.maximum(jnp.abs(den), jnp.exp(-m_t))[..., None]
        b_last = b[..., -1]
        log_w = b_last[..., None] - b + li
        m_new = jnp.maximum(b_last + m_st, jnp.max(log_w, axis=-1))
        w = jnp.exp(log_w - m_new[..., None])
        decay = jnp.exp(b_last + m_st - m_new)
        c_new = decay[..., None, None] * c_st + jnp.einsum('bhs,bhse,bhsd->bhed', w, vc, kc)
        n_new = decay[..., None] * n_st + jnp.einsum('bhs,bhsd->bhd', w, kc)
        return (c_new, n_new, m_new), h

    state, hs = lax.scan(step, state, (qs, ks, vs, log_f, log_i))
    return jnp.moveaxis(hs, 0, 2).reshape(bsz, nh, n, dh), state


def mlstm_branch(q, k, v, o, g, z, q_c, k_c, v_c, o_c, g_c, z_c, conv_w, conv_b, gate_b, norm_w, need_ctx):
    ql, kl, vl, gl = mlstm_inputs(q, k, v, g, conv_w, conv_b, gate_b)
    qc, kc, vc, gc = mlstm_inputs(q_c, k_c, v_c, g_c, conv_w, conv_b, gate_b)
    bsz = ql.shape[0]
    h_lat = None
    h_ctx = None
    for direction in range(2):
        rev = direction == 1
        state0 = (jnp.zeros((bsz, MLSTM_HEADS, MLSTM_HEAD_DIM, MLSTM_HEAD_DIM), jnp.float32),
                  jnp.zeros((bsz, MLSTM_HEADS, MLSTM_HEAD_DIM), jnp.float32),
                  jnp.zeros((bsz, MLSTM_HEADS), jnp.float32))
        hc, st = mlstm_scan(maybe_flip(qc, rev, 2), maybe_flip(kc, rev, 2), maybe_flip(vc, rev, 2),
                            maybe_flip(gc[2 * direction], rev, 2), maybe_flip(gc[2 * direction + 1], rev, 2), state0)
        hl, _ = mlstm_scan(maybe_flip(ql, rev, 2), maybe_flip(kl, rev, 2), maybe_flip(vl, rev, 2),
                           maybe_flip(gl[2 * direction], rev, 2), maybe_flip(gl[2 * direction + 1], rev, 2), st)
        hl = maybe_flip(hl, rev, 2)
        h_lat = hl if h_lat is None else h_lat + hl
        if need_ctx:
            hc = maybe_flip(hc, rev, 2)
            h_ctx = hc if h_ctx is None else h_ctx + hc

    def out(h, og, zg):
        bsz_, nh, n, dh = h.shape
        hn = rmsnorm(h.transpose(0, 2, 1, 3), norm_w.reshape(nh, dh)).reshape(bsz_, n, MLSTM_WIDTH)
        return (jax.nn.sigmoid(og.astype(jnp.float32)) * hn * jax.nn.silu(zg.astype(jnp.float32))).astype(zg.dtype)

    out_ctx = out(h_ctx, o_c, z_c) if need_ctx else None
    return out(h_lat, o, z), out_ctx


def setup_inputs(seed: int = 0) -> dict:
    key = jax.random.key(seed)
    ks = jax.random.split(key, 26)
    f32 = jnp.float32

    def nrm(k, shape, s):
        return s * jax.random.normal(k, shape, f32)

    D = D_MODEL
    f_bias = jnp.linspace(3.0, 6.0, MLSTM_HEADS, dtype=f32)
    zb = jnp.zeros_like(f_bias)
    gate_base = jnp.stack([zb, f_bias, zb, f_bias])
    a_im_base = jnp.pi * jnp.arange(SSM_STATE, dtype=f32)
    return {
        'x': nrm(ks[0], (BATCH, SEQ, D), 1.0),
        'c': nrm(ks[1], (BATCH, D), 1.0),
        'ctx': nrm(ks[2], (BATCH, CTX_LEN, D), 1.0),
        'c_ctx': nrm(ks[3], (D,), 1.0),
        'norm_w': 1.0 + nrm(ks[4], (DEPTH, D), 0.02),
        'ada_w': nrm(ks[5], (DEPTH, D, 3 * D), 0.5 * D ** -0.5),
        'ada_b': nrm(ks[6], (DEPTH, 3 * D), 0.02),
        'w_in': nrm(ks[7], (DEPTH, D, IN_COLS), D ** -0.5),
        'mlstm_gate_b': (gate_base[None] + nrm(ks[8], (DEPTH, 4, MLSTM_HEADS), 0.1)).reshape(DEPTH, N_GATES),
        'ssm_a_re': -0.5 + nrm(ks[9], (DEPTH, 2, SSM_GROUPS, SSM_STATE), 0.01),
        'ssm_a_im': a_im_base + nrm(ks[10], (DEPTH, 2, SSM_GROUPS, SSM_STATE), 0.01),
        'ssm_log_dt': jax.random.uniform(ks[11], (DEPTH, 2, SSM_GROUPS), f32,
                                         minval=math.log(DT_MIN), maxval=math.log(DT_MAX)),
        'ssm_b_re': nrm(ks[12], (DEPTH, SSM_GROUPS, SSM_GROUP, SSM_STATE), (2 * SSM_GROUP) ** -0.5),
        'ssm_b_im': nrm(ks[13], (DEPTH, SSM_GROUPS, SSM_GROUP, SSM_STATE), (2 * SSM_GROUP) ** -0.5),
        'ssm_c_re': nrm(ks[14], (DEPTH, 2, SSM_GROUPS, SSM_GROUP, SSM_STATE), (2 * SSM_STATE) ** -0.5),
        'ssm_c_im': nrm(ks[15], (DEPTH, 2, SSM_GROUPS, SSM_GROUP, SSM_STATE), (2 * SSM_STATE) ** -0.5),
        'ssm_d': nrm(ks[16], (DEPTH, SSM_WIDTH), 1.0),
        'ssm_w_glu': nrm(ks[17], (DEPTH, SSM_WIDTH, SSM_WIDTH), SSM_WIDTH ** -0.5),
        'attn_q_norm': 1.0 + nrm(ks[18], (DEPTH, HEAD_DIM), 0.02),
        'attn_k_norm': 1.0 + nrm(ks[19], (DEPTH, HEAD_DIM), 0.02),
        'mlstm_conv_w': nrm(ks[20], (DEPTH, CONV_WIDTH, 2 * MLSTM_WIDTH), CONV_WIDTH ** -0.5),
        'mlstm_conv_b': nrm(ks[21], (DEPTH, 2 * MLSTM_WIDTH), 0.02),
        'mlstm_norm_w': 1.0 + nrm(ks[22], (DEPTH, MLSTM_WIDTH), 0.02),
        'w_out': nrm(ks[23], (DEPTH, MIX_WIDTH, D), MIX_WIDTH ** -0.5),
        'final_norm_w': 1.0 + nrm(ks[24], (D,), 0.02),
    }


def reference(x, c, ctx, c_ctx, norm_w, ada_w, ada_b, w_in, mlstm_gate_b, ssm_a_re, ssm_a_im, ssm_log_dt,
              ssm_b_re, ssm_b_im, ssm_c_re, ssm_c_im, ssm_d, ssm_w_glu, attn_q_norm, attn_k_norm,
              mlstm_conv_w, mlstm_conv_b, mlstm_norm_w, w_out, final_norm_w):
    n_lat = x.shape[1]
    cos, sin = axial_rope_tables(n_lat)
    c_act = jax.nn.silu(c)
    cc_act = jax.nn.silu(c_ctx)
    h_lat, h_ctx = x, ctx
    for layer in range(DEPTH):
        need_ctx = layer < DEPTH - 1
        shift, scale, gate = jnp.split(c_act @ ada_w[layer] + ada_b[layer], 3, axis=-1)
        shift_c, scale_c, gate_c = jnp.split(cc_act @ ada_w[layer] + ada_b[layer], 3, axis=-1)
        xn = rmsnorm(h_lat, norm_w[layer]) * (1.0 + scale[:, None]) + shift[:, None]
        xc = rmsnorm(h_ctx, norm_w[layer]) * (1.0 + scale_c) + shift_c
        (s5_u, s5_z, at_q, at_k, at_v, at_z,
         ml_q, ml_k, ml_v, ml_o, ml_g, ml_z) = split_sections(xn @ w_in[layer])
        (s5_uc, s5_zc, at_qc, at_kc, at_vc, at_zc,
         ml_qc, ml_kc, ml_vc, ml_oc, ml_gc, ml_zc) = split_sections(xc @ w_in[layer])
        ya, ya_c = s5_branch(s5_u, s5_z, s5_uc, s5_zc, ssm_a_re[layer], ssm_a_im[layer], ssm_log_dt[layer],
                             ssm_b_re[layer], ssm_b_im[layer], ssm_c_re[layer], ssm_c_im[layer],
                             ssm_d[layer], ssm_w_glu[layer], need_ctx)
        yb, yb_c = attention_branch(at_q, at_k, at_v, at_z, at_qc, at_kc, at_vc, at_zc,
                                    attn_q_norm[layer], attn_k_norm[layer], cos, sin, need_ctx)
        yc, yc_c = mlstm_branch(ml_q, ml_k, ml_v, ml_o, ml_g, ml_z, ml_qc, ml_kc, ml_vc, ml_oc, ml_gc, ml_zc,
                                mlstm_conv_w[layer], mlstm_conv_b[layer], mlstm_gate_b[layer],
                                mlstm_norm_w[layer], need_ctx)
        h_lat = h_lat + gate[:, None] * (jnp.concatenate([ya, yb, yc], axis=-1) @ w_out[layer])
        if need_ctx:
            h_ctx = h_ctx + gate_c * (jnp.concatenate([ya_c, yb_c, yc_c], axis=-1) @ w_out[layer])
    return rmsnorm(h_lat, final_norm_w)
```

```python
import numpy as np
import ml_dtypes
import concourse.bass as bass
import concourse.mybir as mybir
from concourse.bass_utils import run_bass_kernel_spmd

F32 = mybir.dt.float32
BF16 = mybir.dt.bfloat16
I32 = mybir.dt.int32
AF = mybir.ActivationFunctionType
ALU = mybir.AluOpType
AX = mybir.AxisListType
NBF = ml_dtypes.bfloat16

ENGS = ['pe', 'act', 'dve', 'pool', 'sp']
N_DMA_SEMS = 24


class Prog:
    def __init__(self, nc, same_engine_sync=True):
        self.nc = nc
        self.ops = {e: [] for e in ENGS}
        self.cnt = {e: 0 for e in ENGS}
        self.waited = {e: {} for e in ENGS}
        self.lastw = {}
        self.readers = {}
        self.dma_cnt = [0] * N_DMA_SEMS
        self.dma_rr = 0
        self.same_engine_sync = same_engine_sync
        self.out_tokens = []
        from contextlib import ExitStack
        self.stack = ExitStack()
        self.nalloc = 0

    def sb(self, shape, dtype, name=None):
        self.nalloc += 1
        return self.stack.enter_context(self.nc.sbuf_tensor('sb_' + (name or ('t%d' % self.nalloc)), list(shape), dtype))

    def ps(self, shape, dtype, name=None):
        self.nalloc += 1
        return self.stack.enter_context(self.nc.psum_tensor('ps_' + (name or ('p%d' % self.nalloc)), list(shape), dtype))

    def _deps(self, reads, writes):
        deps = []
        for r in reads:
            t = self.lastw.get(r)
            if t is not None:
                deps.append(t)
        for w in writes:
            t = self.lastw.get(w)
            if t is not None:
                deps.append(t)
            deps.extend(self.readers.get(w, ()))
        return deps

    def op(self, E, fn, reads=(), writes=(), dma=False, is_output=False):
        reads = list(reads)
        writes = list(writes)
        deps = self._deps(reads, writes)
        waits = []
        if dma:
            idx = self.dma_rr
            self.dma_rr = (self.dma_rr + 1) % N_DMA_SEMS
            if self.dma_cnt[idx] > 0:
                deps.append((('dma', idx), self.dma_cnt[idx]))
            self.dma_cnt[idx] += 16
            token = (('dma', idx), self.dma_cnt[idx])
            inc = (('dma', idx), 16)
        else:
            self.cnt[E] += 1
            token = (('eng', E), self.cnt[E])
            inc = (('eng', E), 1)
        changed = {}
        for (sk, val) in deps:
            if sk == ('eng', E) and not dma and (E == 'pe' or not self.same_engine_sync):
                continue
            changed[sk] = max(changed.get(sk, 0), val)
        for sk, val in changed.items():
            if self.waited[E].get(sk, 0) >= val:
                continue
            self.waited[E][sk] = val
            waits.append((sk, val))
        self.ops[E].append((fn, waits, inc))
        for w in writes:
            self.lastw[w] = token
            self.readers[w] = []
        for r in reads:
            if r in writes:
                continue
            lst = self.readers.setdefault(r, [])
            lst[:] = [t for t in lst if t[0] != token[0]]
            lst.append(token)
        if is_output:
            self.out_tokens.append(token)
        return token

    def dma(self, E, out, in_, reads=(), writes=(), is_output=False, **kw):
        return self.op(E, lambda eng: eng.dma_start(out=out, in_=in_, **kw), reads, writes,
                       dma=True, is_output=is_output)

    def finish(self):
        nc = self.nc
        with self.stack as st:
            sems = {}
            for e in ENGS:
                sems[('eng', e)] = st.enter_context(nc.semaphore('s_' + e))
            for i in range(N_DMA_SEMS):
                sems[('dma', i)] = st.enter_context(nc.semaphore('d_%d' % i))
            block = st.enter_context(nc.Block())
            final_waits = []
            for i in range(N_DMA_SEMS):
                if self.dma_cnt[i] > 0:
                    final_waits.append((('dma', i), self.dma_cnt[i]))
            for e in ENGS:
                if e != 'sp' and self.cnt[e] > 0:
                    final_waits.append((('eng', e), self.cnt[e]))

            def emit(E, eng):
                for (fn, waits, inc) in self.ops[E]:
                    for (sk, val) in waits:
                        eng.wait_ge(sems[sk], val)
                    ins = fn(eng)
                    ins.then_inc(sems[inc[0]], inc[1])
                if E == 'sp':
                    for (sk, val) in final_waits:
                        eng.wait_ge(sems[sk], val)

            @block.tensor
            def _(eng):
                emit('pe', eng)

            @block.scalar
            def _(eng):
                emit('act', eng)

            @block.vector
            def _(eng):
                emit('dve', eng)

            @block.gpsimd
            def _(eng):
                emit('pool', eng)

            @block.sync
            def _(eng):
                emit('sp', eng)


def _mk(method, **kw):
    def fn(eng):
        return getattr(eng, method)(**kw)
    return fn


def _add_helpers():
    def mm(self, out, lhsT, rhs, start, stop, reads, writes):
        return self.op('pe', _mk('matmul', out=out, lhsT=lhsT, rhs=rhs, start=start, stop=stop), reads, writes)

    def tr(self, out, in_, ident, reads, writes):
        return self.op('pe', _mk('transpose', out=out, in_=in_, identity=ident), reads, writes)

    def act(self, out, in_, func, reads, writes, **kw):
        return self.op('act', _mk('activation', out=out, in_=in_, func=func, **kw), reads, writes)

    def tt(self, E, out, in0, in1, op, reads, writes):
        return self.op(E, _mk('tensor_tensor', out=out, in0=in0, in1=in1, op=op), reads, writes)

    def ts(self, E, out, in0, s1, op0, reads, writes, s2=None, op1=None):
        kw = dict(out=out, in0=in0, scalar1=s1, scalar2=s2, op0=op0)
        if op1 is not None:
            kw['op1'] = op1
        return self.op(E, _mk('tensor_scalar', **kw), reads, writes)

    def cp(self, E, out, in_, reads, writes):
        if E == 'act':
            return self.op('act', _mk('copy', out=out, in_=in_), reads, writes)
        return self.op(E, _mk('tensor_copy', out=out, in_=in_), reads, writes)

    def memset(self, E, ap, val, writes):
        return self.op(E, _mk('memset', ap=ap, constant=val), (), writes)

    def gen(self, E, method, reads, writes, **kw):
        return self.op(E, _mk(method, **kw), reads, writes)

    for f in (mm, tr, act, tt, ts, cp, memset, gen):
        setattr(Prog, f.__name__, f)


_add_helpers()


NT = 2112
NTILE = 17
D = 2048
KT = 16
EPS = 1e-6
SECTIONS = [
    ('s5_u', 0, 512, 'cm', 'copy'), ('s5_z', 512, 512, 'cm', 'silu'),
    ('at_q', 1024, 1024, 'tq', None), ('at_k', 2048, 256, 'tk', None), ('at_v', 2304, 256, 'tv', None),
    ('at_z', 2560, 1024, 'cm', 'silu'),
    ('ml_q', 3584, 512, 'cm', 'copy'), ('ml_k', 4096, 512, 'cm', 'copy'), ('ml_v', 4608, 512, 'cm', 'copy'),
    ('ml_o', 5120, 512, 'cm', 'sigmoid'), ('ml_g', 5632, 16, 'cmg', 'copy'), ('ml_z', 5648, 512, 'cm', 'silu'),
]


def tile_rows(t):
    return 64 if t == 0 else 128


def tile_col0(t):
    return 0 if t == 0 else 64 + (t - 1) * 128


GROUPS = [(0, 64)] + [(64 + 512 * g, 512) for g in range(4)]


def build_tpre():
    nc = bass.Bass("TRN2", target_bir_lowering=False)

    def din(name, shape, dt=F32):
        return nc.dram_tensor(name, list(shape), dt, kind="ExternalInput").ap()

    def dout(name, shape, dt=F32):
        return nc.dram_tensor(name, list(shape), dt, kind="ExternalOutput").ap()

    xs = din("xs", [NT, D])
    c2t = din("c2t", [128, 32])
    nwc = din("nwc", [128, 16])
    ada_w = din("ada_w", [D, 6144])
    ada_b = din("ada_b", [6144])
    w_in = din("w_in", [D, 6160])
    qnw = din("qnw", [128])
    knw = din("knw", [128])
    cosd = din("cos", [128, NTILE, 64])
    sind = din("sin", [128, NTILE, 64])
    ident_d = din("ident", [128, 128])

    o_s5u = dout("S5U", [512, NT], BF16)
    o_sz = dout("SZ", [512, NT], BF16)
    o_sza = dout("SZA", [1024, NT], BF16)
    o_mlqkv = dout("MLQKV", [1536, NT], BF16)
    o_so = dout("SO", [512, NT], BF16)
    o_szm = dout("SZM", [512, NT], BF16)
    o_mlg = dout("MLG", [16, NT], F32)
    o_atq = dout("ATQ", [1024, NT], BF16)
    o_atk = dout("ATK", [256, NT], BF16)
    o_atv = dout("ATV", [NT, 256], BF16)
    o_gate = dout("GATE", [2, D], F32)

    P = Prog(nc)
    ident = P.sb([128, 128], F32, 'ident')
    identb = P.sb([128, 128], BF16, 'identb')
    P.dma('sp', ident[:], ident_d[:, :], writes=['ident'])
    P.cp('dve', identb[:], ident[:], ['ident'], ['identb'])
    c2 = P.sb([128, 32], F32, 'c2')
    P.dma('sp', c2[:], c2t[:, :], writes=['c2'])
    cact = P.sb([128, 32], F32, 'cact')
    P.act(cact[:], c2[:], AF.Silu, ['c2'], ['cact'])
    nw = P.sb([128, 16], F32, 'nw')
    P.dma('sp', nw[:], nwc[:, :], writes=['nw'])
    bias2 = [P.sb([2, 256], F32, 'bias2_%d' % i) for i in range(2)]
    ada_b2 = ada_b.partition_broadcast(2)
    qw = P.sb([128, 128], F32, 'qw')
    kw_ = P.sb([128, 128], F32, 'kw')
    P.dma('sp', qw[:], qnw.partition_broadcast(128), writes=['qw'])
    P.dma('sp', kw_[:], knw.partition_broadcast(128), writes=['kw'])
    cos = P.sb([128, NTILE, 64], F32, 'cos')
    sin = P.sb([128, NTILE, 64], F32, 'sin')
    P.dma('sp', cos[:], cosd[:, :, :], writes=['cos'])
    P.dma('sp', sin[:], sind[:, :, :], writes=['sin'])

    ada = P.sb([2, 6144], F32, 'ada')
    wf = [P.sb([128, KT, 256], F32, 'wf%d' % i) for i in range(2)]
    aw = wf
    ps_ada = P.ps([128, 512], F32, 'ps_ada')
    ada_w_v = ada_w.rearrange("(k p) n -> p k n", p=128)
    P.dma('sp', aw[0][:], ada_w_v[:, :, 0:256], writes=[('wf', 0)])
    for ct in range(24):
        b = ct % 2
        if ct + 1 < 24:
            P.dma('sp', aw[1 - b][:], ada_w_v[:, :, (ct + 1) * 256:(ct + 2) * 256], writes=[('wf', 1 - b)])
        for k in range(KT):
            P.mm(ps_ada[0:2, 0:256], cact[:, 2 * k:2 * k + 2], aw[b][:, k, :], k == 0, k == KT - 1,
                 ['cact', ('wf', b)], ['ps_ada'])
        P.dma('sp', bias2[b][:], ada_b2[:, ct * 256:(ct + 1) * 256], writes=[('bias2', b)])
        P.tt('dve', ada[0:2, ct * 256:(ct + 1) * 256], ps_ada[0:2, 0:256], bias2[b][0:2, :],
             ALU.add, ['ps_ada', ('bias2', b)], ['ada'])
    P.dma('sp', o_gate[:, :], ada[0:2, 4096:6144], reads=['ada'], is_output=True)
    ps_col = ps_ada[:, 256:512]
    for j in range(32):
        P.tr(ps_col[:, 2 * j:2 * j + 2], ada[0:2, j * 128:(j + 1) * 128], ident[0:2, 0:2], ['ada', 'ident'], ['ps_ada'])
    shT = P.sb([128, 32], F32, 'shT')
    Acol = P.sb([128, 32], F32, 'Acol')
    P.cp('dve', shT[:], ps_col[:, 0:32], ['ps_ada'], ['shT'])
    P.ts('dve', Acol[:], ps_col[:, 32:64], 1.0, ALU.add, ['ps_ada'], ['Acol'])
    P.tt('dve', Acol[:].rearrange("p (k r) -> p k r", r=2), Acol[:].rearrange("p (k r) -> p k r", r=2),
         nw[:].unsqueeze(2).to_broadcast([128, 16, 2]), ALU.mult, ['Acol', 'nw'], ['Acol'])

    xnT = P.sb([128, KT, NT], BF16, 'xnT')
    xt = [P.sb([128, D], F32, 'xt%d' % i) for i in range(2)]
    junk = P.sb([128, D], BF16, 'junk')
    ss = P.sb([128, 2], F32, 'ss')
    ps_tr = [P.ps([128, 512], F32, 'ps_tr%d' % i) for i in range(2)]
    for t in range(NTILE):
        b = t % 2
        R = tile_rows(t)
        c0 = tile_col0(t)
        r = 1 if t == 0 else 0
        if t == 0:
            P.dma('sp', xt[0][0:64, :], xs[0:64, :], writes=[('xt', 0)])
        if t + 1 < NTILE:
            P.dma('sp', xt[1 - b][0:128, :], xs[tile_col0(t + 1):tile_col0(t + 1) + 128, :], writes=[('xt', 1 - b)])
        P.act(junk[0:R, :], xt[b][0:R, :], AF.Square, [('xt', b)], ['junk', 'ss0'], accum_out=ss[0:R, 0:1])
        P.act(ss[0:R, 1:2], ss[0:R, 0:1], AF.Sqrt, ['ss0'], ['ss1'], scale=1.0 / D, bias=EPS)
        P.gen('dve', 'reciprocal', ['ss1'], ['ss1'], out=ss[0:R, 1:2], in_=ss[0:R, 1:2])
        P.ts('dve', xt[b][0:R, :], xt[b][0:R, :], ss[0:R, 1:2], ALU.mult, [('xt', b), 'ss1'], [('xt', b)])
        for kg in range(4):
            pt = ps_tr[kg % 2]
            pk = ('ps_tr', kg % 2)
            for kk in range(4):
                k = kg * 4 + kk
                P.tr(pt[:, kk * 128:kk * 128 + R], xt[b][0:R, k * 128:(k + 1) * 128], ident[0:R, 0:R],
                     [('xt', b), 'ident'], [pk])
            for kk in range(4):
                k = kg * 4 + kk
                P.act(xnT[:, k, c0:c0 + R], pt[:, kk * 128:kk * 128 + R], AF.Identity, [pk, 'Acol', 'shT'], ['xnT'],
                      scale=Acol[:, 2 * k + r:2 * k + r + 1], bias=shT[:, 2 * k + r:2 * k + r + 1])

    wb = [P.sb([128, KT, 256], BF16, 'wb%d' % i) for i in range(2)]
    ps_cm = [P.ps([128, 512], F32, 'ps_cm%d' % i) for i in range(2)]
    ps_tm = [P.ps([128, 512], F32, 'ps_tm%d' % i) for i in range(2)]
    ps_qt = P.ps([128, 512], BF16, 'ps_qt')
    ost = [P.sb([128, NT], BF16, 'ost%d' % i) for i in range(2)]
    ostg = P.sb([16, NT], F32, 'ostg')
    qf = P.sb([128, 256], F32, 'qf')
    qsq = P.sb([128, 256], F32, 'qsq')
    qss = P.sb([128, 4], F32, 'qss')
    qtmp = P.sb([128, 4, 64], F32, 'qtmp')
    qr = P.sb([128, 256], BF16, 'qr')
    vst = [P.sb([128, 256], BF16, 'vst%d' % i) for i in range(2)]
    w_in_v = w_in.rearrange("(k p) n -> p k n", p=128)
    funcs = {'copy': AF.Copy, 'silu': AF.Silu, 'sigmoid': AF.Sigmoid}
    cm_dest = {'s5_u': (o_s5u, 0), 's5_z': (o_sz, 0), 'at_z': (o_sza, 0), 'ml_q': (o_mlqkv, 0), 'ml_k': (o_mlqkv, 512),
               'ml_v': (o_mlqkv, 1024), 'ml_o': (o_so, 0), 'ml_z': (o_szm, 0)}
    chunk_i = 0
    cm_i = 0
    ost_i = 0
    tm_i = 0
    chunks = []
    for (name, col0, ncols, kind, func) in SECTIONS:
        for cc in range(0, ncols, 256):
            chunks.append((name, col0, ncols, kind, func, cc, min(256, ncols - cc)))

    def load_chunk(i):
        (name_, col0_, ncols_, kind_, func_, cc_, n_) = chunks[i]
        P.dma('sp', wf[i % 2][:, :, 0:n_], w_in_v[:, :, col0_ + cc_:col0_ + cc_ + n_], writes=[('wf', i % 2)])

    load_chunk(0)
    for ci in range(len(chunks)):
        if True:
            (name, col0, ncols, kind, func, cc, n) = chunks[ci]
            b = ci % 2
            if ci + 1 < len(chunks):
                load_chunk(ci + 1)
            P.cp('pool', wb[b][:, 0:8, 0:n], wf[b][:, 0:8, 0:n], [('wf', b)], [('wb', b, 0)])
            P.cp('pool', wb[b][:, 8:16, 0:n], wf[b][:, 8:16, 0:n], [('wf', b)], [('wb', b, 1)])
            wkeys = [('wb', b, 0), ('wb', b, 1)]
            if kind in ('cm', 'cmg'):
                for j in range(0, n, 128):
                    m = min(128, n - j)
                    ob = ost_i % 2
                    ost_i += 1
                    okey = ('ost', ob) if kind == 'cm' else 'ostg'
                    for (g0, gn) in GROUPS:
                        pb = cm_i % 2
                        cm_i += 1
                        for k in range(KT):
                            P.mm(ps_cm[pb][0:m, 0:gn], wb[b][:, k, j:j + m], xnT[:, k, g0:g0 + gn], k == 0, k == KT - 1,
                                 wkeys + ['xnT'], [('ps_cm', pb)])
                        if kind == 'cm':
                            P.act(ost[ob][0:m, g0:g0 + gn], ps_cm[pb][0:m, 0:gn], funcs[func], [('ps_cm', pb)], [okey])
                        else:
                            P.act(ostg[0:m, g0:g0 + gn], ps_cm[pb][0:m, 0:gn], AF.Copy, [('ps_cm', pb)], [okey])
                    if kind == 'cm':
                        dst, roff = cm_dest[name]
                        r0 = roff + cc + j
                        P.dma('sp', dst[r0:r0 + m, :], ost[ob][0:m, :], reads=[okey], is_output=True)
                    else:
                        P.dma('sp', o_mlg[:, :], ostg[0:m, :], reads=[okey], is_output=True)
            else:
                obs = []
                if kind in ('tq', 'tk'):
                    obs = [ost_i % 2, (ost_i + 1) % 2]
                    ost_i += 2
                for t in range(NTILE):
                    R = tile_rows(t)
                    c0 = tile_col0(t)
                    pb = tm_i % 2
                    tm_i += 1
                    pkey = ('ps_tm', pb)
                    for k in range(KT):
                        P.mm(ps_tm[pb][0:R, 0:256], xnT[:, k, c0:c0 + R], wb[b][:, k, 0:256], k == 0, k == KT - 1,
                             wkeys + ['xnT'], [pkey])
                    if kind == 'tv':
                        vb = t % 2
                        P.act(vst[vb][0:R, :], ps_tm[pb][0:R, 0:256], AF.Copy, [pkey], [('vst', vb)])
                        P.dma('sp', o_atv[c0:c0 + R, cc:cc + 256], vst[vb][0:R, :], reads=[('vst', vb)], is_output=True)
                        continue
                    wv = qw if kind == 'tq' else kw_
                    wkey = 'qw' if kind == 'tq' else 'kw'
                    P.act(qf[0:R, :], ps_tm[pb][0:R, 0:256], AF.Copy, [pkey], ['qf'])
                    P.tt('dve', qsq[0:R, :], qf[0:R, :], qf[0:R, :], ALU.mult, ['qf'], ['qsq'])
                    P.gen('dve', 'tensor_reduce', ['qsq'], ['qss0'], out=qss[0:R, 0:2],
                          in_=qsq[0:R, :].rearrange("p (h d) -> p h d", h=2), axis=AX.X, op=ALU.add)
                    P.act(qss[0:R, 2:4], qss[0:R, 0:2], AF.Sqrt, ['qss0'], ['qss1'], scale=1.0 / 128, bias=EPS)
                    P.gen('dve', 'reciprocal', ['qss1'], ['qss1'], out=qss[0:R, 2:4], in_=qss[0:R, 2:4])
                    qf3 = qf[0:R, :].rearrange("p (h d) -> p h d", h=2)
                    P.tt('dve', qf3, qf3, qss[0:R, 2:4].unsqueeze(2).to_broadcast([R, 2, 128]), ALU.mult,
                         ['qf', 'qss1'], ['qf'])
                    P.tt('dve', qf3, qf3, wv[0:R, :].unsqueeze(1).to_broadcast([R, 2, 128]), ALU.mult,
                         ['qf', wkey], ['qf'])
                    x1 = qf3[:, :, 0:64]
                    x2 = qf3[:, :, 64:128]
                    cb = cos[0:R, t, :].unsqueeze(1).to_broadcast([R, 2, 64])
                    sb_ = sin[0:R, t, :].unsqueeze(1).to_broadcast([R, 2, 64])
                    qr3 = qr[0:R, :].rearrange("p (h d) -> p h d", h=2)
                    P.tt('dve', qtmp[0:R, 0:2, :], x1, cb, ALU.mult, ['qf', 'cos'], ['qtmp0'])
                    P.tt('dve', qtmp[0:R, 2:4, :], x2, sb_, ALU.mult, ['qf', 'sin'], ['qtmp1'])
                    P.tt('dve', qr3[:, :, 0:64], qtmp[0:R, 0:2, :], qtmp[0:R, 2:4, :], ALU.subtract,
                         ['qtmp0', 'qtmp1'], ['qr0'])
                    P.tt('dve', qtmp[0:R, 0:2, :], x1, sb_, ALU.mult, ['qf', 'sin', 'qr0'], ['qtmp0'])
                    P.tt('dve', qtmp[0:R, 2:4, :], x2, cb, ALU.mult, ['qf', 'cos', 'qr0'], ['qtmp1'])
                    P.tt('dve', qr3[:, :, 64:128], qtmp[0:R, 0:2, :], qtmp[0:R, 2:4, :], ALU.add,
                         ['qtmp0', 'qtmp1'], ['qr1'])
                    for h in range(2):
                        P.tr(ps_qt[:, h * 128:h * 128 + R], qr[0:R, h * 128:(h + 1) * 128], identb[0:R, 0:R],
                             ['qr0', 'qr1', 'identb'], ['ps_qt'])
                        P.cp('act', ost[obs[h]][:, c0:c0 + R], ps_qt[:, h * 128:h * 128 + R], ['ps_qt'], [('ost', obs[h])])
                dst = o_atq if kind == 'tq' else o_atk
                for h in range(2 if kind in ('tq', 'tk') else 0):
                    r0 = cc + h * 128
                    P.dma('sp', dst[r0:r0 + 128, :], ost[obs[h]][:, :], reads=[('ost', obs[h])], is_output=True)
    P.finish()
    return nc


NS = 8448
NKT = 66


def build_attn():
    nc = bass.Bass("TRN2", target_bir_lowering=False)

    def din(name, shape, dt=F32):
        return nc.dram_tensor(name, list(shape), dt, kind="ExternalInput").ap()

    def dout(name, shape, dt=F32):
        return nc.dram_tensor(name, list(shape), dt, kind="ExternalOutput").ap()

    qt_d = din("QT", [256, NS], BF16)
    kt_d = din("KT", [128, NS], BF16)
    v_d = din("V", [NS, 128], BF16)
    ident_d = din("ident", [128, 128])
    o_d = dout("O", [256, NS], BF16)

    P = Prog(nc)
    identf = P.sb([128, 128], F32, 'identf')
    identb = P.sb([128, 128], BF16, 'identb')
    P.dma('sp', identf[:], ident_d[:, :], writes=['identf'])
    P.cp('dve', identb[:], identf[:], ['identf'], ['identb'])
    QT = [P.sb([128, NS], BF16, 'QT%d' % h) for h in range(2)]
    KT = P.sb([128, NS], BF16, 'KT')
    V1 = P.sb([128, NKT, 132], BF16, 'V1')
    for h in range(2):
        P.dma('sp', QT[h][:], qt_d[h * 128:(h + 1) * 128, :], writes=[('QT', h)])
    P.dma('sp', KT[:], kt_d[:, :], writes=['KT'])
    P.memset('pool', V1[:, :, 128:132], 1.0, ['V1'])
    P.dma('sp', V1[:, :, 0:128], v_d.rearrange("(t p) d -> p t d", p=128), reads=[], writes=['V1'])

    ps_s = [P.ps([128, 512], F32, 'ps_s%d' % i) for i in range(3)]
    ps_o = [P.ps([128, 512], F32, 'ps_o%d' % i) for i in range(4)]
    ps_t = P.ps([128, 512], BF16, 'ps_t')
    pT = [P.sb([128, 512], BF16, 'pT%d' % i) for i in range(3)]
    rec = P.sb([128, 4], F32, 'rec')
    on = P.sb([128, 4, 128], BF16, 'on')
    ost = [P.sb([128, 512], BF16, 'ost%d' % i) for i in range(2)]
    scale = 128 ** -0.5
    si = 0
    gi = 0
    for h in range(2):
        groups = [(0, 256, 0, 2)] + [(256 + 512 * g, 512, 0, NKT) for g in range(16)]
        for (q0, nq, kt0, kt1) in groups:
            nsub = nq // 128
            for kt in range(kt0, kt1):
                sb = si % 3
                si += 1
                P.mm(ps_s[sb][:, 0:nq], KT[:, kt * 128:(kt + 1) * 128], QT[h][:, q0:q0 + nq], True, True,
                     ['KT', ('QT', h)], [('ps_s', sb)])
                P.act(pT[sb][:, 0:nq], ps_s[sb][:, 0:nq], AF.Exp, [('ps_s', sb)], [('pT', sb)], scale=scale)
                for qs in range(nsub):
                    P.mm(ps_o[qs][:, 0:129], pT[sb][:, qs * 128:(qs + 1) * 128], V1[:, kt, 0:129], kt == kt0, kt == kt1 - 1,
                         [('pT', sb), 'V1'], [('ps_o', qs)])
            ob = gi % 2
            gi += 1
            for qs in range(nsub):
                P.gen('dve', 'reciprocal', [('ps_o', qs)], [('rec', qs)], out=rec[:, qs:qs + 1], in_=ps_o[qs][:, 128:129])
                P.ts('dve', on[:, qs, :], ps_o[qs][:, 0:128], rec[:, qs:qs + 1], ALU.mult, [('ps_o', qs), ('rec', qs)], [('on', qs)])
                P.tr(ps_t[:, qs * 128:(qs + 1) * 128], on[:, qs, :], identb[:], [('on', qs), 'identb'], ['ps_t'])
            P.cp('act', ost[ob][:, 0:nq], ps_t[:, 0:nq], ['ps_t'], [('ost', ob)])
            P.dma('sp', o_d[h * 128:(h + 1) * 128, q0:q0 + nq], ost[ob][:, 0:nq], reads=[('ost', ob)], is_output=True)
    P.finish()
    return nc


NS = 8448
TF = 256
NF = NS // TF
TWO_PI = 6.283185
INV_2PI = 0.15915494309189535


def build_s5():
    nc = bass.Bass("TRN2", target_bir_lowering=False)

    def din(name, shape, dt=F32):
        return nc.dram_tensor(name, list(shape), dt, kind="ExternalInput").ap()

    def dout(name, shape, dt=F32):
        return nc.dram_tensor(name, list(shape), dt, kind="ExternalOutput").ap()

    u_d = din("U", [128, NS], BF16)
    are_d = din("ARE", [128, 8])
    aim_d = din("AIM", [128, 8])
    ldt_d = din("LDT", [128, 8])
    bre_d = din("BRE", [128, 4, 128])
    bim_d = din("BIM", [128, 4, 128])
    cre_d = din("CRE", [128, 8, 128])
    cim_d = din("CIM", [128, 8, 128])
    d_d = din("DCOL", [128, 1])
    tau_d = din("TAU", [128, TF])
    g_d = dout("G", [128, NS], BF16)

    P = Prog(nc)
    cnt = [0]

    def T(shape, dt=F32):
        cnt[0] += 1
        return P.sb(shape, dt, 'x%d' % cnt[0])

    U = T([128, NS], BF16)
    UR = T([128, NS], BF16)
    P.dma('sp', U[:], u_d[:, :], writes=['U'])
    are = T([128, 8]); aim = T([128, 8]); ldt = T([128, 8])
    P.dma('sp', are[:], are_d[:, :], writes=['are'])
    P.dma('sp', aim[:], aim_d[:, :], writes=['aim'])
    P.dma('sp', ldt[:], ldt_d[:, :], writes=['ldt'])
    bf = T([128, 4, 128]); bfi = T([128, 4, 128]); cf = T([128, 8, 128]); cfi = T([128, 8, 128])
    P.dma('sp', bf[:], bre_d[:, :, :], writes=['bf'])
    P.dma('sp', bfi[:], bim_d[:, :, :], writes=['bfi'])
    P.dma('sp', cf[:], cre_d[:, :, :], writes=['cf'])
    P.dma('sp', cfi[:], cim_d[:, :, :], writes=['cfi'])
    BRE = T([128, 4, 128], BF16); BIM = T([128, 4, 128], BF16); CRE = T([128, 8, 128], BF16); CIMN = T([128, 8, 128], BF16)
    P.cp('dve', BRE[:], bf[:], ['bf'], ['BRE'])
    P.cp('dve', BIM[:], bfi[:], ['bfi'], ['BIM'])
    P.cp('dve', CRE[:], cf[:], ['cf'], ['CRE'])
    P.ts('dve', CIMN[:], cfi[:], -1.0, ALU.mult, ['cfi'], ['CIMN'])
    dcol = T([128, 1])
    P.dma('sp', dcol[:], d_d[:, :], writes=['dcol'])
    tau = T([128, TF])
    P.dma('sp', tau[:], tau_d[:, :], writes=['tau'])
    P.cp('pool', UR[:, 0:256], U[:, 255::-1] if False else U[:, 0:256][:, ::-1], ['U'], ['UR0'])
    P.cp('pool', UR[:, 256:NS], U[:, 256:NS][:, ::-1], ['U'], ['UR1'])

    sc_ph = T([128, 8 * TF]); sc_ki = T([128, 8 * TF], I32); sc_m = T([128, 8 * TF])

    def sincos(X, F, keyin):
        outs = []
        for off in (0.0, 0.25):
            ph = sc_ph[:, 0:F]; ki = sc_ki[:, 0:F]; m = sc_m[:, 0:F]
            res = T([128, F])
            k = 'sc%d' % cnt[0]
            P.ts('dve', ph, X, INV_2PI, ALU.mult, [keyin], ['sc_ph'], s2=off, op1=ALU.add)
            P.cp('dve', ki, ph, ['sc_ph'], ['sc_ki'])
            P.tt('dve', ph, ph, ki, ALU.subtract, ['sc_ph', 'sc_ki'], ['sc_ph'])
            P.ts('dve', m, ph, 0.5, ALU.is_gt, ['sc_ph'], ['sc_m'])
            P.tt('dve', ph, ph, m, ALU.subtract, ['sc_ph', 'sc_m'], ['sc_ph'])
            P.ts('dve', m, ph, -0.5, ALU.is_lt, ['sc_ph'], ['sc_m'])
            P.tt('dve', ph, ph, m, ALU.add, ['sc_ph', 'sc_m'], ['sc_ph'])
            P.act(res[:], ph, AF.Sin, ['sc_ph'], [k + 'res'], scale=TWO_PI)
            outs.append((res, k + 'res'))
        return outs

    dt = T([128, 8]); th = T([128, 8]); mag = T([128, 8]); tmp = T([128, 8]); tmp2 = T([128, 8])
    P.act(dt[:], ldt[:], AF.Exp, ['ldt'], ['dt'])
    P.tt('dve', th[:], aim[:], dt[:], ALU.mult, ['aim', 'dt'], ['th'])
    P.tt('dve', tmp[:], are[:], dt[:], ALU.mult, ['are', 'dt'], ['tmp'])
    P.act(mag[:], tmp[:], AF.Exp, ['tmp'], ['mag'])
    (sth, ksth), (cth, kcth) = sincos(th[:], 8, 'th')
    lre = T([128, 8]); lim = T([128, 8]); inv = T([128, 8]); fre = T([128, 8]); fim = T([128, 8]); nr = T([128, 8])
    P.tt('dve', lre[:], mag[:], cth[:], ALU.mult, ['mag', kcth], ['lre'])
    P.tt('dve', lim[:], mag[:], sth[:], ALU.mult, ['mag', ksth], ['lim'])
    P.tt('dve', tmp[:], are[:], are[:], ALU.mult, ['are'], ['tmp'])
    P.tt('dve', tmp2[:], aim[:], aim[:], ALU.mult, ['aim'], ['tmp2'])
    P.tt('dve', tmp[:], tmp[:], tmp2[:], ALU.add, ['tmp', 'tmp2'], ['tmp'])
    P.gen('dve', 'reciprocal', ['tmp'], ['inv'], out=inv[:], in_=tmp[:])
    P.ts('dve', nr[:], lre[:], -1.0, ALU.add, ['lre'], ['nr'])
    P.tt('dve', tmp[:], nr[:], are[:], ALU.mult, ['nr', 'are'], ['tmp'])
    P.tt('dve', tmp2[:], lim[:], aim[:], ALU.mult, ['lim', 'aim'], ['tmp2'])
    P.tt('dve', tmp[:], tmp[:], tmp2[:], ALU.add, ['tmp', 'tmp2'], ['tmp'])
    P.tt('dve', fre[:], tmp[:], inv[:], ALU.mult, ['tmp', 'inv'], ['fre'])
    P.tt('dve', tmp[:], lim[:], are[:], ALU.mult, ['lim', 'are'], ['tmp'])
    P.tt('dve', tmp2[:], nr[:], aim[:], ALU.mult, ['nr', 'aim'], ['tmp2'])
    P.tt('dve', tmp[:], tmp[:], tmp2[:], ALU.subtract, ['tmp', 'tmp2'], ['tmp'])
    P.tt('dve', fim[:], tmp[:], inv[:], ALU.mult, ['tmp', 'inv'], ['fim'])
    thF = T([128, 8])
    P.ts('dve', thF[:], th[:], float(TF), ALU.mult, ['th'], ['thF'])
    (sF, ksF), (cF, kcF) = sincos(thF[:], 8, 'thF')
    nsF = T([128, 8])
    P.ts('dve', nsF[:], sF[:], -1.0, ALU.mult, [ksF], ['nsF'])
    ang = T([128, 8, TF])
    for col in range(8):
        P.ts('dve', ang[:, col, :], tau[:], th[:, col:col + 1], ALU.mult, ['tau', 'th'], ['ang'])
    (sT, ksT), (cT, kcT) = sincos(ang[:].rearrange("p c t -> p (c t)"), 8 * TF, 'ang')
    sT3 = sT[:].rearrange("p (c t) -> p c t", c=8)
    cT3 = cT[:].rearrange("p (c t) -> p c t", c=8)
    Tre = T([128, 8, TF]); Tim = T([128, 8, TF]); tq = T([128, TF])
    for col in range(8):
        P.ts('dve', tq[:], sT3[:, col, :], fim[:, col:col + 1], ALU.mult, [ksT, 'fim'], ['tq'])
        P.gen('dve', 'scalar_tensor_tensor', [kcT, 'fre', 'tq'], ['Tre'], out=Tre[:, col, :], in0=cT3[:, col, :],
              scalar=fre[:, col:col + 1], in1=tq[:], op0=ALU.mult, op1=ALU.add)
        P.ts('dve', tq[:], sT3[:, col, :], fre[:, col:col + 1], ALU.mult, [ksT, 'fre'], ['tq'])
        P.gen('dve', 'scalar_tensor_tensor', [kcT, 'fim', 'tq'], ['Tim'], out=Tim[:, col, :], in0=cT3[:, col, :],
              scalar=fim[:, col:col + 1], in1=tq[:], op0=ALU.mult, op1=ALU.subtract)

    yacc = T([128, NS])
    for c in range(NF):
        P.ts('pool', yacc[:, c * TF:(c + 1) * TF], U[:, c * TF:(c + 1) * TF], dcol[:, 0:1], ALU.mult, ['U', 'dcol'], [('yacc', c)])
    ps_w = [P.ps([128, 512], F32, 'ps_w%d' % i) for i in range(2)]
    ps_y = [P.ps([128, 512], F32, 'ps_y%d' % i) for i in range(2)]
    NB = 2
    wsb = [T([128, 2, TF]) for _ in range(NB)]
    A_ = [T([128, 2, TF]) for _ in range(NB)]
    B_ = [T([128, 2, TF]) for _ in range(NB)]
    xt_ = [T([128, 2, TF]) for _ in range(NB)]
    sh = [T([128, 2, TF]) for _ in range(NB)]
    A2 = [T([128, 2, TF]) for _ in range(NB)]
    B2 = [T([128, 2, TF]) for _ in range(NB)]
    sbf = [T([128, 2, TF], BF16) for _ in range(NB)]
    init = [[T([128, 2]) for _ in range(2)] for _ in range(8)]
    itmp = [T([128, 2]) for _ in range(8)]
    it = 0
    yi = 0
    for c in range(NF):
        for d in range(2):
            usrc = U if d == 0 else UR
            ukey = ['U'] if d == 0 else ['UR0', 'UR1']
            yb = yi % 2
            yi += 1
            for j in range(4):
                col = d * 4 + j
                b = it % NB
                it += 1
                k = lambda nm: (nm, b)
                P.mm(ps_w[b][:, 0:TF], BRE[:, j, :], usrc[:, c * TF:(c + 1) * TF], True, True, ['BRE'] + ukey, [k('ps_w')])
                P.mm(ps_w[b][:, TF:2 * TF], BIM[:, j, :], usrc[:, c * TF:(c + 1) * TF], True, True, ['BIM'] + ukey, [k('ps_w')])
                P.cp('act', wsb[b][:].rearrange("p a t -> p (a t)"), ps_w[b][:, :], [k('ps_w')], [k('wsb')])
                P.tt('pool', A_[b][:], wsb[b][:], Tre[:, col:col + 1, :].to_broadcast([128, 2, TF]), ALU.mult, [k('wsb'), 'Tre'], [k('A')])
                P.tt('pool', B_[b][:], wsb[b][:, ::-1, :], Tim[:, col:col + 1, :].to_broadcast([128, 2, TF]), ALU.mult, [k('wsb'), 'Tim'], [k('B')])
                P.tt('pool', xt_[b][:, 0, :], A_[b][:, 0, :], B_[b][:, 0, :], ALU.subtract, [k('A'), k('B')], [k('xt0')])
                P.tt('pool', xt_[b][:, 1, :], A_[b][:, 1, :], B_[b][:, 1, :], ALU.add, [k('A'), k('B')], [k('xt1')])
                par = c % 2
                ikey_prev = ('init', col, 1 - par)
                for ri in range(2):
                    ini = 0.0 if c == 0 else init[col][1 - par][:, ri:ri + 1]
                    rd = [k('xt%d' % ri), 'mag'] + ([] if c == 0 else [ikey_prev])
                    P.gen('dve', 'tensor_tensor_scan', rd, [k('sh%d' % ri)], out=sh[b][:, ri, :],
                          data0=mag[:, col:col + 1].to_broadcast([128, TF]), data1=xt_[b][:, ri, :], initial=ini,
                          op0=ALU.mult, op1=ALU.add)
                if c + 1 < NF:
                    ikey = ('init', col, par)
                    e_re = sh[b][:, 0, TF - 1:TF]
                    e_im = sh[b][:, 1, TF - 1:TF]
                    P.ts('dve', itmp[col][:, 0:1], e_re, cF[:, col:col + 1], ALU.mult, [k('sh0'), kcF], [('itmp', col, 0)])
                    P.gen('dve', 'scalar_tensor_tensor', [k('sh1'), 'nsF', ('itmp', col, 0)], [ikey], out=init[col][par][:, 0:1],
                          in0=e_im, scalar=nsF[:, col:col + 1], in1=itmp[col][:, 0:1], op0=ALU.mult, op1=ALU.add)
                    P.ts('dve', itmp[col][:, 1:2], e_im, cF[:, col:col + 1], ALU.mult, [k('sh1'), kcF], [('itmp', col, 1)])
                    P.gen('dve', 'scalar_tensor_tensor', [k('sh0'), ksF, ('itmp', col, 1), ikey], [ikey], out=init[col][par][:, 1:2],
                          in0=e_re, scalar=sF[:, col:col + 1], in1=itmp[col][:, 1:2], op0=ALU.mult, op1=ALU.add)
                P.tt('dve', A2[b][:], sh[b][:], cT3[:, col:col + 1, :].to_broadcast([128, 2, TF]), ALU.mult, [k('sh0'), k('sh1'), kcT], [k('A2')])
                P.tt('dve', B2[b][:], sh[b][:, ::-1, :], sT3[:, col:col + 1, :].to_broadcast([128, 2, TF]), ALU.mult, [k('sh0'), k('sh1'), ksT], [k('B2')])
                P.tt('dve', sbf[b][:, 0, :], A2[b][:, 0, :], B2[b][:, 0, :], ALU.subtract, [k('A2'), k('B2')], [k('s0')])
                P.tt('dve', sbf[b][:, 1, :], A2[b][:, 1, :], B2[b][:, 1, :], ALU.add, [k('A2'), k('B2')], [k('s1')])
                P.mm(ps_y[yb][:, 0:TF], CRE[:, col, :], sbf[b][:, 0, :], j == 0, False, ['CRE', k('s0')], [('ps_y', yb)])
                P.mm(ps_y[yb][:, 0:TF], CIMN[:, col, :], sbf[b][:, 1, :], False, j == 3, ['CIMN', k('s1')], [('ps_y', yb)])
            if d == 0:
                c0 = c * TF
                P.tt('dve', yacc[:, c0:c0 + TF], yacc[:, c0:c0 + TF], ps_y[yb][:, 0:TF], ALU.add,
                     [('yacc', c), ('ps_y', yb)], [('yacc', c)])
            else:
                if c == 0:
                    o0 = 0
                    oc = 0
                else:
                    o0 = 256 + 8192 - TF * c
                    oc = 1 + (8192 - TF * c) // TF
                P.tt('dve', yacc[:, o0:o0 + TF], yacc[:, o0:o0 + TF], ps_y[yb][:, 0:TF][:, ::-1], ALU.add,
                     [('yacc', oc), ('ps_y', yb)], [('yacc', oc)])
    GC = 1056
    gsq = T([128, GC]); gt = T([128, GC]); gout = [T([128, GC], BF16) for _ in range(2)]
    ykeys = [('yacc', cc) for cc in range(NF)]
    for q4 in range(NS // GC):
        c0 = q4 * GC
        yv = yacc[:, c0:c0 + GC]
        P.tt('dve', gsq[:], yv, yv, ALU.mult, ykeys, ['gsq'])
        P.ts('dve', gsq[:], gsq[:], 0.044715, ALU.mult, ['gsq'], ['gsq'], s2=1.0, op1=ALU.add)
        P.tt('dve', gsq[:], gsq[:], yv, ALU.mult, ['gsq'] + ykeys, ['gsq'])
        P.act(gt[:], gsq[:], AF.Tanh, ['gsq'], ['gt'], scale=0.7978845608028654)
        P.ts('dve', gt[:], gt[:], 0.5, ALU.mult, ['gt'], ['gt'], s2=0.5, op1=ALU.add)
        P.tt('dve', gout[q4 % 2][:], gt[:], yv, ALU.mult, ['gt'] + ykeys, [('gout', q4 % 2)])
        P.dma('sp', g_d[:, c0:c0 + GC], gout[q4 % 2][:], reads=[('gout', q4 % 2)], is_output=True)
    P.finish()
    return nc


NS = 8448
NCH = 66
SEGS = [(0, 256), (256, NS)]


def build_ml():
    nc = bass.Bass("TRN2", target_bir_lowering=False)

    def din(name, shape, dt=F32):
        return nc.dram_tensor(name, list(shape), dt, kind="ExternalInput").ap()

    def dout(name, shape, dt=F32):
        return nc.dram_tensor(name, list(shape), dt, kind="ExternalOutput").ap()

    qkv_d = din("QKV", [2, 3, 128, NS], BF16)
    gi_d = din("GI", [2, NS])
    gf_d = din("GF", [2, NS])
    gb_d = din("GB", [2, 2])
    cw_d = din("CW", [2, 128, 8])
    ident_d = din("ident", [128, 128])
    hm_d = dout("HMD", [2, 128, NS])

    P = Prog(nc)
    cnt = [0]

    def T(shape, dt=F32):
        cnt[0] += 1
        return P.sb(shape, dt, 'x%d' % cnt[0])

    identf = T([128, 128]); identb = T([128, 128], BF16)
    P.dma('sp', identf[:], ident_d[:, :], writes=['identf'])
    P.cp('dve', identb[:], identf[:], ['identf'], ['identb'])
    onesf = T([128, 128]); onesb = T([128, 128], BF16)
    P.memset('dve', onesf[:], 1.0, ['onesf'])
    P.memset('dve', onesb[:], 1.0, ['onesb'])
    maskneg = T([128, 128])
    P.memset('pool', maskneg[:], 0.0, ['maskneg'])
    P.gen('pool', 'affine_select', ['maskneg'], ['maskneg'], out=maskneg[:], in_=maskneg[:], pattern=[[1, 128]],
          compare_op=ALU.is_ge, fill=-1e30, base=0, channel_multiplier=-1)
    Lup = T([128, 128])
    P.memset('pool', Lup[:], 1.0, ['Lup'])
    P.gen('pool', 'affine_select', ['Lup'], ['Lup'], out=Lup[:], in_=Lup[:], pattern=[[1, 128]],
          compare_op=ALU.is_gt, fill=0.0, base=0, channel_multiplier=-1)

    raw = [T([128, NS], BF16) for _ in range(3)]
    qc = T([128, NS], BF16); kc = T([128, NS], BF16)
    PC = 2112
    acc = T([128, PC])
    cw = T([128, 8])
    gb = T([128, 2]); ngb = T([128, 2])
    lf = T([NCH, 128]); li = T([NCH, 128]); bb = T([NCH, 128]); aa = T([NCH, 128]); ca = T([NCH, 128])
    RWE = T([NCH, 384])
    ww = T([NCH, 128])
    colA = T([NCH, 8])
    rowA = T([1, 128]); rowB = T([1, 128])
    acolT = T([128, NCH]); wcolT = T([128, NCH]); dec_bc = T([128, NCH]); ddiag = T([NCH, NCH])
    ST = T([128, 256]); STb = T([128, 256], BF16)
    tmpE = T([128, 128]); E = T([128, 128]); SdT = T([128, 128], BF16); qw = T([128, 128], BF16)
    vtm = T([128, 128], BF16); wk = T([128, 128], BF16); dabs = T([128, 128]); hst = [T([128, 128]) for _ in range(2)]
    ps_bc = P.ps([128, 512], F32, 'ps_bc')
    ps_S = P.ps([128, 512], F32, 'ps_S')
    ps_tr = P.ps([128, 512], BF16, 'ps_tr')
    ps_num = P.ps([128, 512], F32, 'ps_num')
    ps_den = P.ps([128, 512], F32, 'ps_den')
    ps_st = P.ps([128, 512], F32, 'ps_st')
    ps_x = P.ps([128, 512], F32, 'ps_x')
    scale_k = 128 ** -0.5

    for d in range(2):
        for i in range(3):
            P.dma('sp', raw[i][:], qkv_d[d, i, :, :], writes=[('raw', i)])
        P.dma('sp', cw[:], cw_d[d, :, :], writes=['cw'])
        P.dma('sp', gb[:], gb_d[d, :].partition_broadcast(128), writes=['gb'])
        P.ts('dve', ngb[:], gb[:], -1.0, ALU.mult, ['gb'], ['ngb'])
        P.dma('sp', li[:], gi_d[d, :].rearrange("(c t) -> c t", t=128), writes=['li'])
        P.dma('sp', lf[:], gf_d[d, :].rearrange("(c t) -> c t", t=128), writes=['lf'])
        for (src, dst, o, is_k) in ((0, qc, 0, False), (1, kc, 4, True)):
            x = raw[src]
            dkey = 'kc' if is_k else 'qc'
            for (s0, s1) in SEGS:
                for p0 in range(s0, s1, PC):
                    p1 = min(s1, p0 + PC)
                    n = p1 - p0
                    P.act(acc[:, 0:n], x[:, p0:p1], AF.Identity, [('raw', src), 'cw'], ['acc'],
                          scale=cw[:, o + 1:o + 2], bias=cw[:, o + 3:o + 4])
                    lo = max(p0, s0 + 1)
                    P.gen('dve', 'scalar_tensor_tensor', [('raw', src), 'cw', 'acc'], ['acc'], out=acc[:, lo - p0:n],
                          in0=x[:, lo - 1:p1 - 1], scalar=cw[:, o:o + 1], in1=acc[:, lo - p0:n], op0=ALU.mult, op1=ALU.add)
                    hi = min(p1, s1 - 1)
                    P.gen('dve', 'scalar_tensor_tensor', [('raw', src), 'cw', 'acc'], ['acc'], out=acc[:, 0:hi - p0],
                          in0=x[:, p0 + 1:hi + 1], scalar=cw[:, o + 2:o + 3], in1=acc[:, 0:hi - p0], op0=ALU.mult, op1=ALU.add)
                    if is_k:
                        P.act(acc[:, 0:n], acc[:, 0:n], AF.Silu, ['acc'], ['acc'])
                        P.ts('dve', dst[:, p0:p1], acc[:, 0:n], scale_k, ALU.mult, ['acc'], [dkey])
                    else:
                        P.act(dst[:, p0:p1], acc[:, 0:n], AF.Silu, ['acc'], [dkey])
        P.act(lf[:], lf[:], AF.Exp, ['lf', 'ngb'], ['lf'], scale=-1.0, bias=ngb[0:NCH, 1:2])
        P.act(lf[:], lf[:], AF.Ln, ['lf'], ['lf'], bias=1.0)
        P.ts('dve', lf[:], lf[:], -1.0, ALU.mult, ['lf'], ['lf'])
        P.ts('dve', li[:], li[:], gb[0:NCH, 0:1], ALU.add, ['li', 'gb'], ['li'])
        P.gen('dve', 'tensor_tensor_scan', ['lf', 'onesf'], ['bb'], out=bb[:], data0=onesf[0:NCH, 0:128], data1=lf[:],
              initial=0.0, op0=ALU.mult, op1=ALU.add)
        P.mm(ps_x[0:NCH, 0:1], Lup[0:NCH, 0:NCH], bb[:, 127:128], True, True, ['Lup', 'bb'], ['ps_x'])
        P.cp('dve', colA[:, 0:1], ps_x[0:NCH, 0:1], ['ps_x'], ['colA0'])
        P.ts('dve', bb[:], bb[:], colA[:, 0:1], ALU.add, ['bb', 'colA0'], ['bb'])
        P.tt('dve', aa[:], li[:], bb[:], ALU.subtract, ['li', 'bb'], ['aa'])
        P.gen('dve', 'tensor_tensor_scan', ['aa', 'onesf'], ['ca'], out=ca[:], data0=aa[:], data1=onesf[0:NCH, 0:128],
              initial=-1e30, op0=ALU.max, op1=ALU.mult)
        P.tr(ps_x[0:1, 64:64 + NCH], ca[:, 127:128], identf[0:NCH, 0:NCH], ['ca', 'identf'], ['ps_x'])
        P.cp('dve', rowA[0:1, 0:NCH], ps_x[0:1, 64:64 + NCH], ['ps_x'], ['rowA'])
        P.memset('dve', rowB[0:1, 0:1], 0.0, ['rowB'])
        P.gen('dve', 'tensor_tensor_scan', ['rowA', 'onesf', 'rowB'], ['rowB'], out=rowB[0:1, 1:NCH + 1], data0=rowA[0:1, 0:NCH],
              data1=onesf[0:1, 0:NCH], initial=0.0, op0=ALU.max, op1=ALU.mult)
        P.tr(ps_x[0:NCH, 2:3], rowB[0:1, 0:NCH], identf[0:1, 0:1], ['rowB', 'identf'], ['ps_x'])
        P.cp('dve', colA[:, 1:2], ps_x[0:NCH, 2:3], ['ps_x'], ['colA1'])
        P.ts('dve', ca[:], ca[:], colA[:, 1:2], ALU.max, ['ca', 'colA1'], ['ca'])
        P.ts('dve', RWE[:, 0:128], ca[:], -1.0, ALU.mult, ['ca'], ['RWE0'])
        P.act(RWE[:, 128:256], ca[:], AF.Exp, ['ca', 'colA1'], ['RWE1'], scale=-1.0, bias=colA[:, 1:2])
        P.tt('dve', ww[:], bb[:], ca[:], ALU.add, ['bb', 'ca'], ['ww'])
        P.act(RWE[:, 256:384], ww[:], AF.Exp, ['ww'], ['RWE2'], scale=-1.0)
        P.ts('dve', colA[:, 2:3], ca[:, 127:128], -1.0, ALU.mult, ['ca'], ['colA2'])
        P.act(ww[:], aa[:], AF.Exp, ['aa', 'colA2', 'RWE2'], ['ww'], bias=colA[:, 2:3])
        P.act(colA[:, 3:4], colA[:, 1:2], AF.Exp, ['colA1', 'colA2'], ['colA3'], bias=colA[:, 2:3])
        P.tr(ps_x[:, 128:128 + NCH], aa[:], identf[0:NCH, 0:NCH], ['aa', 'identf'], ['ps_x'])
        P.cp('dve', acolT[:], ps_x[:, 128:128 + NCH], ['ps_x'], ['acolT'])
        P.tr(ps_x[:, 256:256 + NCH], ww[:], identf[0:NCH, 0:NCH], ['ww', 'identf'], ['ps_x'])
        P.cp('dve', wcolT[:], ps_x[:, 256:256 + NCH], ['ps_x'], ['wcolT'])
        P.ts('dve', ddiag[:], identf[0:NCH, 0:NCH], colA[:, 3:4], ALU.mult, ['identf', 'colA3'], ['ddiag'])
        P.mm(ps_x[:, 384:384 + NCH], onesf[0:NCH, :], ddiag[:], True, True, ['onesf', 'ddiag'], ['ps_x'])
        P.cp('dve', dec_bc[:], ps_x[:, 384:384 + NCH], ['ps_x'], ['dec_bc'])
        P.memset('dve', ST[:], 0.0, ['ST'])
        P.memset('dve', STb[:], 0.0, ['STb'])
        rwe_keys = ['RWE0', 'RWE1', 'RWE2']
        for c in range(NCH):
            cs = slice(c * 128, (c + 1) * 128)
            P.mm(ps_bc[:, 0:384], identf[0:NCH, c:c + 1].to_broadcast([NCH, 128]), RWE[:, 0:384], True, True,
                 ['identf'] + rwe_keys, ['ps_bc'])
            P.mm(ps_S[:, 0:128], kc[:, cs], qc[:, cs], True, True, ['kc', 'qc'], ['ps_S'])
            P.tr(ps_tr[:, 0:128], kc[:, cs], identb[:], ['kc', 'identb'], ['ps_tr'])
            P.tr(ps_tr[:, 128:256], raw[2][:, cs], identb[:], [('raw', 2), 'identb'], ['ps_tr'])
            P.gen('dve', 'scalar_tensor_tensor', ['ps_bc', 'acolT', 'maskneg'], ['tmpE'], out=tmpE[:], in0=ps_bc[:, 0:128],
                  scalar=acolT[:, c:c + 1], in1=maskneg[:], op0=ALU.add, op1=ALU.add)
            P.act(E[:], tmpE[:], AF.Exp, ['tmpE'], ['E'])
            P.tt('dve', SdT[:], ps_S[:, 0:128], E[:], ALU.mult, ['ps_S', 'E'], ['SdT'])
            P.tt('dve', qw[:], qc[:, cs], ps_bc[:, 128:256], ALU.mult, ['qc', 'ps_bc'], ['qw'])
            P.cp('act', vtm[:], ps_tr[:, 128:256], ['ps_tr'], ['vtm'])
            P.ts('dve', wk[:], ps_tr[:, 0:128], wcolT[:, c:c + 1], ALU.mult, ['ps_tr', 'wcolT'], ['wk'])
            P.mm(ps_num[:, 0:128], vtm[:], SdT[:], True, False, ['vtm', 'SdT'], ['ps_num'])
            P.mm(ps_num[:, 0:128], STb[:, 0:128], qw[:], False, True, ['STb', 'qw'], ['ps_num'])
            P.mm(ps_den[:, 0:128], onesb[:], SdT[:], True, False, ['onesb', 'SdT'], ['ps_den'])
            P.mm(ps_den[:, 0:128], STb[:, 128:256], qw[:], False, True, ['STb', 'qw'], ['ps_den'])
            P.act(dabs[:], ps_den[:, 0:128], AF.Abs, ['ps_den'], ['dabs'])
            P.tt('dve', dabs[:], dabs[:], ps_bc[:, 256:384], ALU.max, ['dabs', 'ps_bc'], ['dabs'])
            P.gen('dve', 'reciprocal', ['dabs'], ['dabs'], out=dabs[:], in_=dabs[:])
            hb = c % 2
            P.tt('dve', hst[hb][:], ps_num[:, 0:128], dabs[:], ALU.mult, ['ps_num', 'dabs'], [('hst', hb)])
            P.dma('sp', hm_d[d, :, cs], hst[hb][:], reads=[('hst', hb)], is_output=True)
            if c + 1 < NCH:
                P.mm(ps_st[:, 0:128], wk[:], vtm[:], True, True, ['wk', 'vtm'], ['ps_st'])
                P.mm(ps_st[:, 128:256], wk[:], onesb[:], True, True, ['wk', 'onesb'], ['ps_st'])
                P.gen('dve', 'scalar_tensor_tensor', ['ST', 'dec_bc', 'ps_st'], ['ST'], out=ST[:], in0=ST[:],
                      scalar=dec_bc[:, c:c + 1], in1=ps_st[:, 0:256], op0=ALU.mult, op1=ALU.add)
                P.cp('act', STb[:], ST[:], ['ST'], ['STb'])
    P.finish()
    return nc


NT = 2112
D = 2048
KT = 16
EPS = 1e-6
GROUPS_PO = [(0, 64, 1)] + [(64 + 512 * g, 512, 0) for g in range(4)]


def build_tpost(final):
    nc = bass.Bass("TRN2", target_bir_lowering=False)

    def din(name, shape, dt=F32):
        return nc.dram_tensor(name, list(shape), dt, kind="ExternalInput").ap()

    def dout(name, shape, dt=F32):
        return nc.dram_tensor(name, list(shape), dt, kind="ExternalOutput").ap()

    h_d = din("H", [NT, D])
    g_d = din("G", [512, NT], BF16)
    sz_d = din("SZ", [512, NT], BF16)
    o_d = din("O", [1024, NT], BF16)
    sza_d = din("SZA", [1024, NT], BF16)
    hm_d = din("HM", [512, NT])
    hmb_d = din("HMB", [512, NT])
    so_d = din("SO", [512, NT], BF16)
    szm_d = din("SZM", [512, NT], BF16)
    mnw_d = din("MNW", [128, 4])
    wglu_d = din("WGLU", [512, 512])
    wout_d = din("WOUT", [D, D])
    gate_d = din("GATE", [2, D])
    fw_d = din("FW", [D])
    out_d = dout("HN", [NT, D])

    P = Prog(nc)
    cnt = [0]

    def T(shape, dt=F32):
        cnt[0] += 1
        return P.sb(shape, dt, 'x%d' % cnt[0])

    wst = [T([128, KT, 64]) for _ in range(2)]
    woutb = T([128, KT, D], BF16)
    wout_v = wout_d.rearrange("(k p) n -> p k n", p=128)
    P.dma('sp', wst[0][:], wout_v[:, :, 0:64], writes=[('wst', 0)])
    for ci in range(32):
        b = ci % 2
        if ci + 1 < 32:
            P.dma('sp', wst[1 - b][:], wout_v[:, :, (ci + 1) * 64:(ci + 2) * 64], writes=[('wst', 1 - b)])
        P.cp('pool', woutb[:, :, ci * 64:(ci + 1) * 64], wst[b][:], [('wst', b)], ['woutb'])
    wgf = T([128, 4, 512])
    wglu = T([128, 4, 512], BF16)
    P.dma('sp', wgf[:], wglu_d.rearrange("(k p) n -> p k n", p=128), writes=['wgf'])
    P.cp('dve', wglu[:], wgf[:], ['wgf'], ['wglu'])
    gate_bc = [T([128, D]) for _ in range(2)]
    for r in range(2):
        P.dma('sp', gate_bc[r][:], gate_d[r, :].partition_broadcast(128), writes=[('gate_bc', r)])
    if final:
        fw_bc = T([128, D])
        P.dma('sp', fw_bc[:], fw_d.partition_broadcast(128), writes=['fw_bc'])
    mnw = T([128, 4])
    P.dma('sp', mnw[:], mnw_d[:, :], writes=['mnw'])
    onesb = T([128, 128], BF16)
    P.memset('dve', onesb[:], 1.0, ['onesb'])

    Gt = T([128, 4, 512], BF16); SZt = T([128, 4, 512], BF16)
    Ot = T([128, 8, 512], BF16); SZAt = T([128, 8, 512], BF16)
    HMt = T([128, 4, 512]); HMBt = T([128, 4, 512]); SOt = T([128, 4, 512], BF16); SZMt = T([128, 4, 512], BF16)
    YT = T([128, KT, 512], BF16)
    sig = T([128, 512]); t1 = T([128, 512]); t2 = T([128, 512]); hsq = T([128, 512], BF16); rstd = T([128, 512])
    ht = [T([128, D]) for _ in range(2)]
    hn1 = T([128, D])
    hn = [hn1, hn1]
    tg = T([128, 512])
    ss = T([128, 2])
    ps_g = [P.ps([128, 512], F32, 'ps_g%d' % i) for i in range(2)]
    ps_o = [P.ps([128, 512], F32, 'ps_o%d' % i) for i in range(4)]
    gi = 0
    oi = 0
    hi = 0
    for (c0, gn, grow) in GROUPS_PO:
        cs = slice(c0, c0 + gn)
        P.dma('sp', Gt[:, :, 0:gn], g_d.rearrange("(k p) n -> p k n", p=128)[:, :, cs], writes=['Gt'])
        P.dma('sp', SZt[:, :, 0:gn], sz_d.rearrange("(k p) n -> p k n", p=128)[:, :, cs], writes=['SZt'])
        P.dma('sp', Ot[:, :, 0:gn], o_d.rearrange("(k p) n -> p k n", p=128)[:, :, cs], writes=['Ot'])
        P.dma('sp', SZAt[:, :, 0:gn], sza_d.rearrange("(k p) n -> p k n", p=128)[:, :, cs], writes=['SZAt'])
        P.dma('sp', HMt[:, :, 0:gn], hm_d.rearrange("(k p) n -> p k n", p=128)[:, :, cs], writes=['HMt'])
        P.dma('sp', HMBt[:, :, 0:gn], hmb_d.rearrange("(k p) n -> p k n", p=128)[:, :, cs], writes=['HMBt'])
        P.tt('pool', HMt[:, :, 0:gn], HMt[:, :, 0:gn], HMBt[:, :, 0:gn], ALU.add, ['HMt', 'HMBt'], ['HMt'])
        P.dma('sp', SOt[:, :, 0:gn], so_d.rearrange("(k p) n -> p k n", p=128)[:, :, cs], writes=['SOt'])
        P.dma('sp', SZMt[:, :, 0:gn], szm_d.rearrange("(k p) n -> p k n", p=128)[:, :, cs], writes=['SZMt'])
        for j in range(4):
            pb = gi % 2
            gi += 1
            for i in range(4):
                P.mm(ps_g[pb][:, 0:gn], wglu[:, i, j * 128:(j + 1) * 128], Gt[:, i, 0:gn], i == 0, i == 3, ['wglu', 'Gt'], [('ps_g', pb)])
            P.act(sig[:, 0:gn], ps_g[pb][:, 0:gn], AF.Sigmoid, [('ps_g', pb)], ['sig'])
            P.tt('dve', t1[:, 0:gn], Gt[:, j, 0:gn], SZt[:, j, 0:gn], ALU.mult, ['Gt', 'SZt'], ['t1'])
            P.tt('dve', YT[:, j, 0:gn], t1[:, 0:gn], sig[:, 0:gn], ALU.mult, ['t1', 'sig'], [('YT', j)])
        for j in range(8):
            P.tt('pool', YT[:, 4 + j, 0:gn], Ot[:, j, 0:gn], SZAt[:, j, 0:gn], ALU.mult, ['Ot', 'SZAt'], [('YT', 4 + j)])
        for j in range(4):
            pb = gi % 2
            gi += 1
            P.tt('dve', hsq[:, 0:gn], HMt[:, j, 0:gn], HMt[:, j, 0:gn], ALU.mult, ['HMt'], ['hsq'])
            P.mm(ps_g[pb][:, 0:gn], onesb[:], hsq[:, 0:gn], True, True, ['onesb', 'hsq'], [('ps_g', pb)])
            P.act(rstd[:, 0:gn], ps_g[pb][:, 0:gn], AF.Sqrt, [('ps_g', pb)], ['rstd'], scale=1.0 / 128, bias=EPS)
            P.gen('dve', 'reciprocal', ['rstd'], ['rstd'], out=rstd[:, 0:gn], in_=rstd[:, 0:gn])
            P.tt('dve', t1[:, 0:gn], HMt[:, j, 0:gn], rstd[:, 0:gn], ALU.mult, ['HMt', 'rstd'], ['t1'])
            P.tt('dve', t2[:, 0:gn], SOt[:, j, 0:gn], SZMt[:, j, 0:gn], ALU.mult, ['SOt', 'SZMt'], ['t2'])
            P.gen('dve', 'scalar_tensor_tensor', ['t1', 'mnw', 't2'], [('YT', 12 + j)], out=YT[:, 12 + j, 0:gn], in0=t1[:, 0:gn],
                  scalar=mnw[:, j:j + 1], in1=t2[:, 0:gn], op0=ALU.mult, op1=ALU.mult)
        ykeys = [('YT', k) for k in range(KT)]
        for s0 in range(0, gn, 128):
            R = min(128, gn - s0)
            hb = hi % 2
            hi += 1
            P.dma('sp', ht[hb][0:R, :], h_d[c0 + s0:c0 + s0 + R, :], writes=[('ht', hb)])
            for ct in range(4):
                ob = oi % 4
                oi += 1
                for k in range(KT):
                    P.mm(ps_o[ob][0:R, :], YT[:, k, s0:s0 + R], woutb[:, k, ct * 512:(ct + 1) * 512], k == 0, k == KT - 1,
                         ykeys + ['woutb'], [('ps_o', ob)])
                P.tt('dve', tg[0:R, :], ps_o[ob][0:R, :], gate_bc[grow][0:R, ct * 512:(ct + 1) * 512], ALU.mult,
                     [('ps_o', ob), ('gate_bc', grow)], ['tg'])
                P.tt('dve', hn[hb][0:R, ct * 512:(ct + 1) * 512], tg[0:R, :], ht[hb][0:R, ct * 512:(ct + 1) * 512], ALU.add,
                     ['tg', ('ht', hb)], ['hn'])
            if final:
                P.act(ht[hb][0:R, :], hn[hb][0:R, :], AF.Square, ['hn'], [('ht', hb), 'ss0'], accum_out=ss[0:R, 0:1])
                P.act(ss[0:R, 1:2], ss[0:R, 0:1], AF.Sqrt, ['ss0'], ['ss1'], scale=1.0 / D, bias=EPS)
                P.gen('dve', 'reciprocal', ['ss1'], ['ss1'], out=ss[0:R, 1:2], in_=ss[0:R, 1:2])
                P.gen('dve', 'scalar_tensor_tensor', ['hn', 'ss1', 'fw_bc'], ['hn'], out=hn[hb][0:R, :], in0=hn[hb][0:R, :],
                      scalar=ss[0:R, 1:2], in1=fw_bc[0:R, :], op0=ALU.mult, op1=ALU.mult)
            P.dma('sp', out_d[c0 + s0:c0 + s0 + R, :], hn[hb][0:R, :], reads=['hn'], is_output=True)
    P.finish()
    return nc


def seq_cm(outs, key, b):
    parts = [outs[b * 4 + r][key] for r in range(4)]
    return np.concatenate([p[:, 0:64] for p in parts] + [p[:, 64:] for p in parts], axis=1)

def seq_tm(outs, key, b):
    parts = [outs[b * 4 + r][key] for r in range(4)]
    return np.concatenate([p[0:64] for p in parts] + [p[64:] for p in parts], axis=0)

def slice_cm(full, r):
    return np.ascontiguousarray(np.concatenate([full[:, 64 * r:64 * r + 64], full[:, 256 + 2048 * r:256 + 2048 * (r + 1)]], axis=1))

def rev_seg(a):
    return np.concatenate([a[..., 0:256][..., ::-1], a[..., 256:][..., ::-1]], axis=-1)

def ml_inputs(inp, layer, tp, core):
    b, q = core // 4, core % 4
    qkv = seq_cm(tp, 'MLQKV', b)
    f = np.stack([qkv[128 * q:128 * (q + 1)], qkv[512 + 128 * q:512 + 128 * (q + 1)], qkv[1024 + 128 * q:1024 + 128 * (q + 1)]])
    QKV = np.ascontiguousarray(np.stack([f, rev_seg(f)]))
    g = seq_cm(tp, 'MLG', b)
    GI = np.ascontiguousarray(np.stack([g[0 * 4 + q], rev_seg(g[2 * 4 + q])]))
    GF = np.ascontiguousarray(np.stack([g[1 * 4 + q], rev_seg(g[3 * 4 + q])]))
    gbv = inp['mlstm_gate_b'][layer]
    GB = np.array([[gbv[0 * 4 + q], gbv[1 * 4 + q]], [gbv[2 * 4 + q], gbv[3 * 4 + q]]], np.float32)
    cwv = inp['mlstm_conv_w'][layer]; cbv = inp['mlstm_conv_b'][layer]
    CW = np.zeros((2, 128, 8), np.float32)
    for d in range(2):
        taps = [0, 1, 2] if d == 0 else [2, 1, 0]
        for o, base in ((0, 128 * q), (4, 512 + 128 * q)):
            for j, tp_ in enumerate(taps):
                CW[d, :, o + j] = cwv[tp_, base:base + 128]
            CW[d, :, o + 3] = cbv[base:base + 128]
    return dict(QKV=QKV, GI=GI, GF=GF, GB=GB, CW=CW, ident=np.eye(128, dtype=np.float32))

def ml_combine(HMD):
    return HMD[0], np.ascontiguousarray(rev_seg(HMD[1]))


def _rope_tables_np(n_tokens):
    rows = n_tokens // 64
    row = np.broadcast_to(np.arange(rows, dtype=np.float32)[:, None], (rows, 64)).reshape(-1)
    col = np.broadcast_to(np.arange(64, dtype=np.float32)[None, :], (rows, 64)).reshape(-1)
    n_freq = 32
    inv = (10000.0 ** (-np.arange(n_freq, dtype=np.float32) / n_freq)).astype(np.float32)
    ang = np.concatenate([row[:, None] * inv, col[:, None] * inv], axis=-1).astype(np.float32)
    return np.cos(ang).astype(np.float32), np.sin(ang).astype(np.float32)


_ROPE = None
_IDENT = np.eye(128, dtype=np.float32)


def tpre_inputs(inp, layer, hl, hc, core):
    global _ROPE
    if _ROPE is None:
        _ROPE = _rope_tables_np(8192)
    cos, sin = _ROPE
    b, r = core // 4, core % 4
    xs = np.concatenate([hc[b, 64 * r:64 * r + 64], hl[b, 2048 * r:2048 * r + 2048]], axis=0)
    c2 = np.stack([inp['c'][b], inp['c_ctx']], axis=0)
    c2t = np.ascontiguousarray(c2.reshape(2, 16, 128).transpose(2, 1, 0).reshape(128, 32))
    nwc = np.ascontiguousarray(inp['norm_w'][layer].reshape(16, 128).T)
    cs = np.zeros((128, 17, 64), np.float32)
    sn = np.zeros((128, 17, 64), np.float32)
    cs[:, 0, :] = 1.0
    cs[:, 1:, :] = cos[2048 * r:2048 * r + 2048].reshape(16, 128, 64).transpose(1, 0, 2)
    sn[:, 1:, :] = sin[2048 * r:2048 * r + 2048].reshape(16, 128, 64).transpose(1, 0, 2)
    return dict(xs=np.ascontiguousarray(xs, dtype=np.float32), c2t=c2t, nwc=nwc, ada_w=inp['ada_w'][layer], ada_b=inp['ada_b'][layer],
                w_in=inp['w_in'][layer], qnw=inp['attn_q_norm'][layer], knw=inp['attn_k_norm'][layer],
                cos=cs, sin=sn, ident=_IDENT)


def s5_inputs(inp, layer, tp, core):
    b, q = core // 4, core % 4
    U = np.ascontiguousarray(seq_cm(tp, 'S5U', b)[128 * q:128 * (q + 1)])

    def col8(a):
        out = np.zeros((128, 8), np.float32)
        for d in range(2):
            for j in range(4):
                for g2 in range(2):
                    out[g2 * 64:(g2 + 1) * 64, d * 4 + j] = a[d, 8 * q + 2 * j + g2]
        return out
    ARE = col8(inp['ssm_a_re'][layer])
    AIM = col8(inp['ssm_a_im'][layer])
    LDT = col8(np.broadcast_to(inp['ssm_log_dt'][layer][:, :, None], (2, 32, 64)))
    BRE = np.zeros((128, 4, 128), np.float32)
    BIM = np.zeros((128, 4, 128), np.float32)
    CRE = np.zeros((128, 8, 128), np.float32)
    CIM = np.zeros((128, 8, 128), np.float32)
    for j in range(4):
        for g2 in range(2):
            gloc = 2 * j + g2
            g = 8 * q + gloc
            BRE[gloc * 16:(gloc + 1) * 16, j, g2 * 64:(g2 + 1) * 64] = inp['ssm_b_re'][layer][g]
            BIM[gloc * 16:(gloc + 1) * 16, j, g2 * 64:(g2 + 1) * 64] = inp['ssm_b_im'][layer][g]
            for d in range(2):
                CRE[g2 * 64:(g2 + 1) * 64, d * 4 + j, gloc * 16:(gloc + 1) * 16] = inp['ssm_c_re'][layer][d, g].T
                CIM[g2 * 64:(g2 + 1) * 64, d * 4 + j, gloc * 16:(gloc + 1) * 16] = inp['ssm_c_im'][layer][d, g].T
    DCOL = np.ascontiguousarray(inp['ssm_d'][layer][128 * q:128 * (q + 1)].reshape(128, 1))
    TAU = np.ascontiguousarray(np.broadcast_to(np.arange(256, dtype=np.float32)[None, :], (128, 256)))
    return dict(U=U, ARE=ARE, AIM=AIM, LDT=LDT, BRE=BRE, BIM=BIM, CRE=CRE, CIM=CIM, DCOL=DCOL, TAU=TAU)


def attn_inputs(tp, core):
    b, q = core // 4, core % 4
    QT = np.ascontiguousarray(seq_cm(tp, 'ATQ', b)[256 * q:256 * (q + 1)])
    kv = q // 2
    KT_ = np.ascontiguousarray(seq_cm(tp, 'ATK', b)[128 * kv:128 * (kv + 1)])
    V = np.ascontiguousarray(seq_tm(tp, 'ATV', b)[:, 128 * kv:128 * (kv + 1)])
    return dict(QT=QT, KT=KT_, V=V, ident=_IDENT)


def tpost_inputs(inp, layer, H, G, O, HMF, HMB, tp, core):
    b, r = core // 4, core % 4
    return dict(H=H, G=slice_cm(G, r), SZ=tp[core]['SZ'], O=slice_cm(O, r), SZA=tp[core]['SZA'],
                HM=slice_cm(HMF, r), HMB=slice_cm(HMB, r), SO=tp[core]['SO'], SZM=tp[core]['SZM'],
                MNW=np.ascontiguousarray(inp['mlstm_norm_w'][layer].reshape(4, 128).T),
                WGLU=inp['ssm_w_glu'][layer], WOUT=inp['w_out'][layer], GATE=tp[core]['GATE'], FW=inp['final_norm_w'])


_NC_CACHE = {}


def _get_nc(name):
    if name not in _NC_CACHE:
        _NC_CACHE[name] = {'tpre': build_tpre, 's5': build_s5, 'attn': build_attn, 'ml': build_ml,
                           'tpost0': lambda: build_tpost(False), 'tpost1': lambda: build_tpost(True)}[name]()
    return _NC_CACHE[name]


def _run(name, in_maps):
    nc = _get_nc(name)
    res = run_bass_kernel_spmd(nc, in_maps, core_ids=list(range(8)))
    return [{k: np.asarray(v) for k, v in r.items()} for r in res.results]


def kernel(**inputs):
    inp = {k: np.asarray(v) for k, v in inputs.items()}
    hl = np.ascontiguousarray(inp['x'], dtype=np.float32)
    hc = np.ascontiguousarray(inp['ctx'], dtype=np.float32)
    cores = list(range(8))
    for layer in range(2):
        tp = _run('tpre', [tpre_inputs(inp, layer, hl, hc, c) for c in cores])
        s5 = _run('s5', [s5_inputs(inp, layer, tp, c) for c in cores])
        at = _run('attn', [attn_inputs(tp, c) for c in cores])
        ml = _run('ml', [ml_inputs(inp, layer, tp, c) for c in cores])
        post_in = []
        for b in range(2):
            G = np.concatenate([s5[b * 4 + q]['G'] for q in range(4)], axis=0)
            O = np.concatenate([at[b * 4 + q]['O'] for q in range(4)], axis=0)
            hmf = np.concatenate([ml[b * 4 + q]['HMD'][0] for q in range(4)], axis=0)
            hmb = np.concatenate([rev_seg(ml[b * 4 + q]['HMD'][1]) for q in range(4)], axis=0)
            for r in range(4):
                c = b * 4 + r
                H = np.ascontiguousarray(np.concatenate([hc[b, 64 * r:64 * r + 64], hl[b, 2048 * r:2048 * r + 2048]], axis=0))
                post_in.append(tpost_inputs(inp, layer, H, G, O, hmf, hmb, tp, c))
        po = _run('tpost1' if layer == 1 else 'tpost0', post_in)
        hl = np.empty_like(hl)
        hc = np.empty_like(hc)
        for c in cores:
            b, r = c // 4, c % 4
            hn = po[c]['HN']
            hc[b, 64 * r:64 * r + 64] = hn[0:64]
            hl[b, 2048 * r:2048 * r + 2048] = hn[64:]
    return hl
```
